# Optimizing a Trainium2 kernel written in Bass

```python
import jax, jax.numpy as jnp
from jax import lax
import numpy as np

D_MODEL = 4096
BATCH = 2
SEQ = 4096
DEPTH = 2
DEC_BATCH = 32
DEC_SEQ = 32
PAST_LEN = 2048

CHUNK = 64
N_MIXERS = 2
N_A_LAYERS = (DEPTH + 1) // 2
N_B_LAYERS = DEPTH // 2
HG_HEAD_K = 128
HG_HEADS = D_MODEL // HG_HEAD_K
HG_HEAD_V = D_MODEL // HG_HEADS
HG_DK = HG_HEADS * HG_HEAD_K
HG_DV = HG_HEADS * HG_HEAD_V
RW_HEAD = 64
RW_HEADS = D_MODEL // RW_HEAD
RW_DECAY_LORA = 128
RW_AAA_LORA = 128
RW_GATE_LORA = 480
D_FF = 11008
CONV_W = 3
PLE_DIM = 256
RMS_EPS = 1e-6
GN_EPS = 64e-5

kernel_name = 'hgrn2_rwkv7_convffn_stream_step'


def rmsnorm(x, g):
    xf = x.astype(jnp.float32)
    y = xf * lax.rsqrt(jnp.mean(xf * xf, axis=-1, keepdims=True) + RMS_EPS)
    return (y * g.astype(jnp.float32)).astype(x.dtype)


def to_chunks(a, c):
    b, t, h, d = a.shape
    return a.reshape(b, t // c, c, h, d).transpose(1, 0, 3, 2, 4)


def hgrn2_chunk_scan(q, k, v, logf, s0):
    b, t, h, _ = q.shape
    c = min(CHUNK, t)
    mask = jnp.tril(jnp.ones((c, c), dtype=bool))[None, None, :, :, None]

    def step(S, blk):
        qb, kb, vb, gb = blk
        G = jnp.cumsum(gb, axis=2)
        diff = G[:, :, :, None, :] - G[:, :, None, :, :]
        decay_ts = jnp.exp(jnp.where(mask, diff, -jnp.inf))
        A = jnp.einsum('bhtk,bhsk,bhtsk->bhts', qb, kb, decay_ts)
        o = (jnp.einsum('bhts,bhsv->bhtv', A, vb)
             + jnp.einsum('bhtk,bhkv->bhtv', qb * jnp.exp(G), S))
        G_end = G[:, :, -1, :]
        S = (S * jnp.exp(G_end)[..., None]
             + jnp.einsum('bhsk,bhsv->bhkv', kb * jnp.exp(G_end[:, :, None, :] - G), vb))
        return S, o

    S, o = lax.scan(step, s0, (to_chunks(q, c), to_chunks(k, c), to_chunks(v, c), to_chunks(logf, c)))
    o = o.transpose(1, 0, 3, 2, 4).reshape(b, t, h, v.shape[-1])
    return o, S


def hgrn2_mixer(xn, s0, w_in, lb, g_norm, w_o):
    b, t, _ = xn.shape
    q, f, i, g = jnp.split(xn @ w_in, [HG_DK, 2 * HG_DK, 2 * HG_DK + HG_DV], axis=-1)
    lbf = lb.astype(jnp.float32)
    fg = lbf + (1.0 - lbf) * jax.nn.sigmoid(f.astype(jnp.float32))
    logf = jnp.log(fg)
    k = 1.0 - fg
    qf = jax.nn.silu(q.astype(jnp.float32))
    hk = lambda a: a.reshape(b, t, HG_HEADS, HG_HEAD_K)
    o, s_new = hgrn2_chunk_scan(hk(qf), hk(k), i.astype(jnp.float32).reshape(b, t, HG_HEADS, HG_HEAD_V),
                                hk(logf), s0.astype(jnp.float32))
    o = rmsnorm(o, g_norm).reshape(b, t, HG_DV)
    o = (o * jax.nn.sigmoid(g.astype(jnp.float32))).astype(xn.dtype)
    return o @ w_o, s_new


def rwkv7_scan(r, w, k, v, kk, a, s0):
    def step(S, inp):
        r_, w_, k_, v_, kk_, a_ = inp
        s_kk = jnp.einsum('bhij,bhj->bhi', S, kk_)
        S = (S * w_[:, :, None, :] - s_kk[..., None] * (kk_ * a_)[:, :, None, :]
             + v_[..., None] * k_[:, :, None, :])
        return S, jnp.einsum('bhij,bhj->bhi', S, r_)

    tm = lambda z: jnp.swapaxes(z, 0, 1)
    S, y = lax.scan(step, s0, (tm(r), tm(w), tm(k), tm(v), tm(kk), tm(a)))
    return tm(y), S


def rwkv7_mixer(xn, shift0, s0, mu, w_rkv, w0, w1, w2, a0, a1, a2, g1, g2,
                k_k, k_a, r_k, lnx_w, lnx_b, w_o):
    b, t, d = xn.shape
    x_prev = jnp.concatenate([shift0[:, None, :].astype(xn.dtype), xn[:, :-1]], axis=1)
    xx = x_prev - xn
    xr, xw, xk, xv, xa, xg = (xn + xx * mu[j] for j in range(6))
    wr, wk, wv = jnp.split(w_rkv, 3, axis=1)
    r, k, v = xr @ wr, xk @ wk, xv @ wv
    w_log = -jax.nn.softplus(-(w0 + jnp.tanh(xw @ w1) @ w2).astype(jnp.float32)) - 0.5
    decay = jnp.exp(-jnp.exp(w_log))
    a = jax.nn.sigmoid((a0 + (xa @ a1) @ a2).astype(jnp.float32))
    g = jax.nn.sigmoid(xg @ g1) @ g2
    hs = lambda z: z.astype(jnp.float32).reshape(b, t, RW_HEADS, RW_HEAD)
    r_h, k_h, v_h, a_h, w_h = hs(r), hs(k), hs(v), hs(a), hs(decay)
    kk = k_h * hs(k_k)[0, 0] if False else k_h * k_k.astype(jnp.float32).reshape(RW_HEADS, RW_HEAD)
    kk = kk / jnp.maximum(jnp.sqrt(jnp.sum(kk * kk, axis=-1, keepdims=True)), 1e-12)
    k_h = k_h * (1.0 + (a_h - 1.0) * k_a.astype(jnp.float32).reshape(RW_HEADS, RW_HEAD))
    y, s_new = rwkv7_scan(r_h, w_h, k_h, v_h, kk, a_h, s0.astype(jnp.float32))
    mean = jnp.mean(y, axis=-1, keepdims=True)
    var = jnp.mean(jnp.square(y - mean), axis=-1, keepdims=True)
    y = ((y - mean) * lax.rsqrt(var + GN_EPS)).reshape(b, t, d) * lnx_w + lnx_b
    bonus = jnp.sum(r_h * k_h * r_k.astype(jnp.float32), axis=-1, keepdims=True) * v_h
    y = y + bonus.reshape(b, t, d)
    return (y * g).astype(xn.dtype) @ w_o, xn[:, -1], s_new


def conv_ffn(xn, c0, w_up, conv_w, conv_b, w_down):
    t = xn.shape[1]
    hg, hu = jnp.split(xn @ w_up, 2, axis=-1)
    hpad = jnp.concatenate([c0.astype(hg.dtype), hg], axis=1)
    hc = conv_b + sum(conv_w[j] * hpad[:, j:j + t] for j in range(CONV_W))
    return (jax.nn.gelu(hc, approximate=False) * hu) @ w_down, hpad[:, t:]


def run_trunk(h, p, s_hg, s_rw, s_shift, s_conv,
              norm_mix, norm_ffn, norm_ple, norm_final,
              hg_w_in, hg_lb_logits, hg_gnorm, hg_w_o,
              rw_mu, rw_w_rkv, rw_w0, rw_w1, rw_w2, rw_a0, rw_a1, rw_a2,
              rw_g1, rw_g2, rw_k_k, rw_k_a, rw_r_k, rw_lnx_w, rw_lnx_b, rw_w_o,
              ffn_w_up, ffn_conv_w, ffn_conv_b, ffn_w_down,
              ple_w_proj, ple_w_gate):
    lb_all = jnp.cumsum(jax.nn.softmax(hg_lb_logits.astype(jnp.float32), axis=0), axis=0)
    new_hg, new_rw, new_shift, new_conv = [], [], [], []
    for i in range(DEPTH):
        j = i // N_MIXERS
        xn = rmsnorm(h, norm_mix[i])
        if i % N_MIXERS == 0:
            mix, s = hgrn2_mixer(xn, s_hg[j], hg_w_in[j], lb_all[j], hg_gnorm[j], hg_w_o[j])
            new_hg.append(s)
        else:
            mix, sh, s = rwkv7_mixer(xn, s_shift[j], s_rw[j], rw_mu[j], rw_w_rkv[j], rw_w0[j], rw_w1[j],
                                     rw_w2[j], rw_a0[j], rw_a1[j], rw_a2[j], rw_g1[j], rw_g2[j],
                                     rw_k_k[j], rw_k_a[j], rw_r_k[j], rw_lnx_w[j], rw_lnx_b[j], rw_w_o[j])
            new_rw.append(s)
            new_shift.append(sh)
        h = h + mix
        f, c = conv_ffn(rmsnorm(h, norm_ffn[i]), s_conv[i], ffn_w_up[i], ffn_conv_w[i], ffn_conv_b[i], ffn_w_down[i])
        h = h + f
        new_conv.append(c)
        gate = jax.nn.sigmoid(rmsnorm(h, norm_ple[i]) @ ple_w_gate[i])
        h = h + gate * (p[i] @ ple_w_proj[i])
    y = rmsnorm(h, norm_final)
    return y, jnp.stack(new_hg), jnp.stack(new_rw), jnp.stack(new_shift), jnp.stack(new_conv)


def setup_inputs(seed: int = 0) -> dict:
    key = jax.random.key(seed)
    ks = jax.random.split(key, 38)
    nrm = lambda i, shape, scale: scale * jax.random.normal(ks[i], shape, jnp.float32)
    D = D_MODEL
    return {
        'x_prompt': nrm(0, (BATCH, SEQ, D), 1.0),
        'x_sample': nrm(1, (DEC_BATCH, DEC_SEQ, D), 1.0),
        'p_prompt': nrm(2, (DEPTH, BATCH, SEQ, PLE_DIM), 1.0),
        'p_sample': nrm(3, (DEPTH, DEC_BATCH, DEC_SEQ, PLE_DIM), 1.0),
        'state_hgrn': nrm(4, (N_A_LAYERS, DEC_BATCH, HG_HEADS, HG_HEAD_K, HG_HEAD_V), 0.5),
        'state_rwkv': nrm(5, (N_B_LAYERS, DEC_BATCH, RW_HEADS, RW_HEAD, RW_HEAD), 0.1),
        'state_shift': nrm(6, (N_B_LAYERS, DEC_BATCH, D), 1.0),
        'state_ffn_conv': nrm(7, (DEPTH, DEC_BATCH, CONV_W - 1, D_FF), 1.0),
        'norm_mix': 1.0 + nrm(8, (DEPTH, D), 0.02),
        'norm_ffn': 1.0 + nrm(9, (DEPTH, D), 0.02),
        'norm_ple': 1.0 + nrm(10, (DEPTH, D), 0.02),
        'norm_final': 1.0 + nrm(11, (D,), 0.02),
        'hg_w_in': nrm(12, (N_A_LAYERS, D, 3 * HG_DK + HG_DV), D ** -0.5),
        'hg_lb_logits': nrm(13, (N_A_LAYERS + 1, HG_DK), 0.5),
        'hg_gnorm': 1.0 + nrm(14, (N_A_LAYERS, HG_HEAD_V), 0.02),
        'hg_w_o': nrm(15, (N_A_LAYERS, HG_DV, D), HG_DV ** -0.5),
        'rw_mu': jax.random.uniform(ks[16], (N_B_LAYERS, 6, D), jnp.float32),
        'rw_w_rkv': nrm(17, (N_B_LAYERS, D, 3 * D), D ** -0.5),
        'rw_w0': -1.0 + nrm(18, (N_B_LAYERS, D), 0.5),
        'rw_w1': nrm(19, (N_B_LAYERS, D, RW_DECAY_LORA), D ** -0.5),
        'rw_w2': nrm(20, (N_B_LAYERS, RW_DECAY_LORA, D), 0.5 * RW_DECAY_LORA ** -0.5),
        'rw_a0': nrm(21, (N_B_LAYERS, D), 0.1),
        'rw_a1': nrm(22, (N_B_LAYERS, D, RW_AAA_LORA), D ** -0.5),
        'rw_a2': nrm(23, (N_B_LAYERS, RW_AAA_LORA, D), RW_AAA_LORA ** -0.5),
        'rw_g1': nrm(24, (N_B_LAYERS, D, RW_GATE_LORA), D ** -0.5),
        'rw_g2': nrm(25, (N_B_LAYERS, RW_GATE_LORA, D), RW_GATE_LORA ** -0.5),
        'rw_k_k': 0.85 + nrm(26, (N_B_LAYERS, D), 0.05),
        'rw_k_a': 1.0 + nrm(27, (N_B_LAYERS, D), 0.05),
        'rw_r_k': nrm(28, (N_B_LAYERS, RW_HEADS, RW_HEAD), 0.1),
        'rw_lnx_w': 1.0 + nrm(29, (N_B_LAYERS, D), 0.02),
        'rw_lnx_b': nrm(30, (N_B_LAYERS, D), 0.02),
        'rw_w_o': nrm(31, (N_B_LAYERS, D, D), D ** -0.5),
        'ffn_w_up': nrm(32, (DEPTH, D, 2 * D_FF), D ** -0.5),
        'ffn_conv_w': nrm(33, (DEPTH, CONV_W, D_FF), CONV_W ** -0.5),
        'ffn_conv_b': nrm(34, (DEPTH, D_FF), 0.02),
        'ffn_w_down': nrm(35, (DEPTH, D_FF, D), D_FF ** -0.5),
        'ple_w_proj': nrm(36, (DEPTH, PLE_DIM, D), PLE_DIM ** -0.5),
        'ple_w_gate': nrm(37, (DEPTH, D, D), D ** -0.5),
    }


def reference(x_prompt, x_sample, p_prompt, p_sample, state_hgrn, state_rwkv, state_shift, state_ffn_conv,
              norm_mix, norm_ffn, norm_ple, norm_final,
              hg_w_in, hg_lb_logits, hg_gnorm, hg_w_o,
              rw_mu, rw_w_rkv, rw_w0, rw_w1, rw_w2, rw_a0, rw_a1, rw_a2,
              rw_g1, rw_g2, rw_k_k, rw_k_a, rw_r_k, rw_lnx_w, rw_lnx_b, rw_w_o,
              ffn_w_up, ffn_conv_w, ffn_conv_b, ffn_w_down,
              ple_w_proj, ple_w_gate):
    weights = (norm_mix, norm_ffn, norm_ple, norm_final,
               hg_w_in, hg_lb_logits, hg_gnorm, hg_w_o,
               rw_mu, rw_w_rkv, rw_w0, rw_w1, rw_w2, rw_a0, rw_a1, rw_a2,
               rw_g1, rw_g2, rw_k_k, rw_k_a, rw_r_k, rw_lnx_w, rw_lnx_b, rw_w_o,
               ffn_w_up, ffn_conv_w, ffn_conv_b, ffn_w_down,
               ple_w_proj, ple_w_gate)
    b = x_prompt.shape[0]
    z_hg = jnp.zeros((N_A_LAYERS, b, HG_HEADS, HG_HEAD_K, HG_HEAD_V), jnp.float32)
    z_rw = jnp.zeros((N_B_LAYERS, b, RW_HEADS, RW_HEAD, RW_HEAD), jnp.float32)
    z_sh = jnp.zeros((N_B_LAYERS, b, D_MODEL), x_prompt.dtype)
    z_cv = jnp.zeros((DEPTH, b, CONV_W - 1, D_FF), x_prompt.dtype)
    y_prompt, hg_p, rw_p, sh_p, cv_p = run_trunk(x_prompt, p_prompt, z_hg, z_rw, z_sh, z_cv, *weights)
    y_sample, hg_s, rw_s, sh_s, cv_s = run_trunk(x_sample, p_sample, state_hgrn, state_rwkv, state_shift,
                                                 state_ffn_conv, *weights)
    return (y_prompt, y_sample, hg_p, rw_p, sh_p, cv_p, hg_s, rw_s, sh_s, cv_s)
```

```python
import numpy as np
from contextlib import ExitStack
import concourse.bass as bass
import concourse.mybir as mybir
from concourse.bass_utils import run_bass_kernel_spmd

F32 = mybir.dt.float32
BF16 = mybir.dt.bfloat16
AF = mybir.ActivationFunctionType
ALU = mybir.AluOpType
AX = mybir.AxisListType


class Sched:
    STREAMS = ("pe", "act", "dve", "pool", "sp")
    NROT = 4

    def __init__(self, nc, es):
        self.nc = nc
        self.es = es
        self.ops = {s: [] for s in self.STREAMS}
        self.sems = {}
        self.count = {}
        self.inc = {}
        for s in ("pe", "act", "dve"):
            self._mksem(s, 1)
        for s in ("pool", "sp", "act"):
            for r in range(self.NROT):
                self._mksem(f"{s}_d{r}", 16)
        self.rot = {"pool": 0, "sp": 0, "act": 0}
        self.seen = {s: {} for s in self.STREAMS}
        self.lastw = {}
        self.readers = {}
        self.nops = 0
        self.pending = {s: {} for s in self.STREAMS}

    def barrier(self):
        snap = {k: v for k, v in self.count.items() if v > 0}
        for s in self.STREAMS:
            self.pending[s] = dict(snap)

    def _mksem(self, name, inc):
        self.sems[name] = self.es.enter_context(self.nc.semaphore(name))
        self.count[name] = 0
        self.inc[name] = inc

    def sb(self, name, shape, dt):
        return self.es.enter_context(self.nc.sbuf_tensor(name, shape, dt))

    def ps(self, name, shape, dt):
        return self.es.enter_context(self.nc.psum_tensor(name, shape, dt))

    def _add(self, stream, sem, fn, reads, writes):
        waits = {}
        seen = self.seen[stream]

        def need(dep):
            if dep is None:
                return
            x, n = dep
            if x == "pe" and stream == "pe":
                return
            if seen.get(x, 0) >= n:
                return
            if waits.get(x, 0) < n:
                waits[x] = n

        if self.pending[stream]:
            for x, n in self.pending[stream].items():
                need((x, n))
            self.pending[stream] = {}
        for k in reads:
            need(self.lastw.get(k))
        for k in writes:
            need(self.lastw.get(k))
            for r in self.readers.get(k, ()):
                need(r)
        for x, n in waits.items():
            seen[x] = n
        self.count[sem] += 1
        me = (sem, self.count[sem])
        self.ops[stream].append((tuple(waits.items()), fn, sem))
        for k in writes:
            self.lastw[k] = me
            self.readers[k] = []
        for k in reads:
            if k in writes:
                continue
            lst = self.readers.setdefault(k, [])
            for i, (x, n) in enumerate(lst):
                if x == sem:
                    lst[i] = me
                    break
            else:
                lst.append(me)
        self.nops += 1
        return me

    def op(self, stream, fn, reads=(), writes=()):
        return self._add(stream, stream, fn, reads, writes)

    def dma(self, stream, out, in_, reads=(), writes=(), **kw):
        r = self.rot[stream]
        self.rot[stream] = (r + 1) % self.NROT
        sem = f"{stream}_d{r}"
        return self._add(stream, sem, lambda e: e.dma_start(out=out, in_=in_, **kw), reads, writes)

    def finish(self):
        nc = self.nc
        final = [(s, self.count[s]) for s in self.sems if self.inc[s] == 16 and self.count[s] > 0]
        with nc.Block() as block:
            def emit(stream):
                def body(e):
                    for waits, fn, sem in self.ops[stream]:
                        for x, n in waits:
                            e.wait_ge(self.sems[x], n * self.inc[x])
                        fn(e).then_inc(self.sems[sem], self.inc[sem])
                    if stream == "sp":
                        for s, n in final:
                            e.wait_ge(self.sems[s], n * 16)
                return body
            block.tensor(emit("pe"))
            block.vector(emit("dve"))
            block.scalar(emit("act"))
            block.gpsimd(emit("pool"))
            block.sync(emit("sp"))


D = 4096
KC = 32
DFF = 11008
FCH = 86
T = 512
RS = 4
LS = 32
TSUB = 128
FG = 16
FGROUPS = [(0, 16), (16, 32), (32, 48), (48, 64), (64, 80), (80, 86)]
RMS_EPS = 1e-6
GN_EPS = 64e-5
WDEC = 0.6065306597126334

(V_NMIX0, V_NMIX1, V_NFFN0, V_NFFN1, V_NPLE0, V_NPLE1, V_NFIN, V_LB0, V_LB1,
 V_MUR, V_MUW, V_MUK, V_MUV, V_MUA, V_MUG, V_W0, V_A0, V_KK, V_KA, V_LNW, V_LNB, V_RK) = range(22)
NV = 22


def build_program(NPT, stop_after=None):
    nc = bass.Bass("TRN2", target_bir_lowering=False)

    def din(name, shape):
        return nc.dram_tensor(name, list(shape), F32, kind="ExternalInput").ap()

    def dout(name, shape):
        return nc.dram_tensor(name, list(shape), F32, kind="ExternalOutput").ap()

    xT_p = din("xT_p", [NPT, 128, KC * T])
    xT_s = din("xT_s", [128, KC * TSUB])
    pT_p = din("pT_p", [2, NPT, 128, 2 * T])
    pT_s = din("pT_s", [2, 128, 2 * TSUB])
    st_hg_s = din("st_hg_s", [32, 128, RS * 128])
    st_rw_s = din("st_rw_s", [32, 128, RS * 64])
    st_sh_s = din("st_sh_s", [128, RS * 32])
    st_cv_s = din("st_cv_s", [2, 128, FCH * RS * 2])
    w_in = din("w_in", [128, 128, KC * 128])
    w_o0 = din("w_o0", [2, 32, 128, 16 * 128])
    w_rkv = din("w_rkv", [96, 128, KC * 128])
    w_l1 = din("w_l1", [6, 128, KC * 128])
    w_sw = din("w_sw", [32, 128, 6 * 128])
    w_o1 = din("w_o1", [32, 128, KC * 128])
    w_up = din("w_up", [2, 2 * FCH, 128, KC * 128])
    w_dnA = din("w_dnA", [2, 5, 32, 128, FG * 128])
    w_dnB = din("w_dnB", [2, 32, 128, 6 * 128])
    w_pg = din("w_pg", [2, 32, 128, KC * 128])
    w_pp = din("w_pp", [2, 32, 128, 2 * 128])
    vec_d = din("vec", [128, NV * 32])
    gn_d = din("gnorm", [128, 1])
    cvp_d = din("convp", [128, 2 * FCH * 4])
    c_onesD = din("c_onesD", [128, 128])
    c_ones128 = din("c_ones128", [128, 128])
    c_bones = din("c_bones", [128, 128])
    c_ident = din("c_ident", [128, 128])
    c_mask = din("c_mask", [128, 6 * 64])
    c_cm = din("c_cm", [128, T + TSUB])

    yT_p = dout("yT_p", [NPT, 128, KC * T])
    yT_s = dout("yT_s", [128, KC * TSUB])
    o_hg_p = dout("o_hg_p", [32, 128, 128])
    o_hg_s = dout("o_hg_s", [32, 128, RS * 128])
    o_rw_p = dout("o_rw_p", [32, 128, 64])
    o_rw_s = dout("o_rw_s", [32, 128, RS * 64])
    o_sh_p = dout("o_sh_p", [128, 32])
    o_sh_s = dout("o_sh_s", [128, RS * 32])
    o_cv_p = dout("o_cv_p", [2, 128, FCH * 2])
    o_cv_s = dout("o_cv_s", [2, 128, FCH * RS * 2])

    with ExitStack() as es:
        S = Sched(nc, es)
        h = S.sb("h", [128, KC, T], F32)
        XB = S.sb("XB", [128, KC * T], BF16)
        OB = S.sb("OB", [128, FG * T], BF16)
        NWR = 3
        WR = [S.sb(f"WR{i}", [128, KC, 128], BF16) for i in range(NWR)]
        NTF = 10
        TF = [S.sb(f"TF{i}", [128, T], F32) for i in range(NTF)]
        NTB = 6
        TB = [S.sb(f"TB{i}", [128, T], BF16) for i in range(NTB)]
        vec = S.sb("vecs", [128, NV, 32], F32)
        dvec = S.sb("dvec", [128, 3, 32], F32)
        gn = S.sb("gn", [128, 1], F32)
        cvp = S.sb("cvp", [128, 2, FCH, 4], F32)
        cstP = S.sb("cstP", [128, 2, FCH, 1, 2], F32)
        cstS = S.sb("cstS", [128, 2, FCH, RS, 2], F32)
        onesD = S.sb("onesD", [128, 128], F32)
        ones128 = S.sb("ones128", [128, 128], F32)
        bones = S.sb("bones", [128, 128], F32)
        ident = S.sb("ident", [128, 128], BF16)
        masks = S.sb("masks", [128, 6, 64], F32)
        cm = S.sb("cm", [128, T + TSUB], F32)
        pT = S.sb("pT", [128, 2, T], BF16)
        PW = [S.sb(f"PW{i}", [128, 2, 128], BF16) for i in range(2)]
        SW = [S.sb(f"SW{i}", [128, 6, 128], BF16) for i in range(2)]
        xlP = S.sb("xlP", [128, 1, 32], F32)
        xlS = S.sb("xlS", [128, RS, 32], F32)
        xlN = S.sb("xlN", [128, RS, 32], F32)
        hid = S.sb("hid", [128, 6, TSUB], BF16)
        HGP = [S.sb(f"HGP{i}", [128, T + 2 * RS], F32) for i in range(2)]
        VK = [S.sb(f"VK{i}", [64, 256], BF16) for i in range(2)]
        ATt = [S.sb(f"ATt{i}", [64, 64], BF16) for i in range(2)]
        Sst = S.sb("Sst", [128, RS, 128], F32)
        Sbf = S.sb("Sbf", [128, RS, 128], BF16)
        NM = S.sb("NM", [128, 64], F32)
        NMb = S.sb("NMb", [128, 64], BF16)
        MR = S.sb("MR", [128, 64], BF16)
        N0 = [S.sb(f"N0_{i}", [64, 64], F32) for i in range(2)]
        NT_ = [S.sb(f"NT_{i}", [64, 64], F32) for i in range(2)]
        Wt = [S.sb(f"Wt{i}", [64, 64], F32) for i in range(2)]
        BK = S.sb("BK", [128, 64], BF16)
        UV = S.sb("UV", [128, 64], BF16)
        Hst = S.sb("Hst", [128, RS, 64], F32)
        Hbf = S.sb("Hbf", [128, RS, 64], BF16)
        BKT = S.sb("BKT", [128, 4 * 128], BF16)
        VT = S.sb("VT", [128, 4 * 128], BF16)
        PG = [S.ps(f"PG{i}", [128, 512], F32) for i in range(4)]
        pO = S.ps("pO", [128, 512], F32)
        pS = S.ps("pS", [128, 512], F32)
        pS2 = S.ps("pS2", [128, 512], F32)
        pT_ = S.ps("pTr", [128, 1024], BF16)

        def mm(out, lhsT, rhs, start=True, stop=True, reads=(), writes=()):
            S.op("pe", lambda e: e.matmul(out, lhsT, rhs, start=start, stop=stop), reads, writes)

        def tr(out, in_, idn, reads=(), writes=()):
            S.op("pe", lambda e: e.transpose(out, in_, idn), reads, writes)

        def act(out, in_, func, reads=(), writes=(), bias=None, scale=None):
            kw = {}
            if bias is not None:
                kw["bias"] = bias
            if scale is not None:
                kw["scale"] = scale
            S.op("act", lambda e: e.activation(out, in_, func, **kw), reads, writes)

        def tt(out, in0, in1, op, reads=(), writes=()):
            S.op("dve", lambda e: e.tensor_tensor(out, in0, in1, op), reads, writes)

        def ts(out, in0, s1, s2, op0, op1=None, reads=(), writes=()):
            if op1 is None:
                S.op("dve", lambda e: e.tensor_scalar(out, in0, s1, None, op0), reads, writes)
            else:
                S.op("dve", lambda e: e.tensor_scalar(out, in0, s1, s2, op0, op1), reads, writes)

        def stt(out, in0, scalar, in1, op0, op1, reads=(), writes=()):
            S.op("dve", lambda e: e.scalar_tensor_tensor(out, in0, scalar, in1, op0, op1), reads, writes)

        def vcopy(out, in_, reads=(), writes=()):
            S.op("dve", lambda e: e.tensor_copy(out, in_), reads, writes)

        def vmemset(out, val, writes=()):
            S.op("dve", lambda e: e.memset(out, val), (), writes)

        def vrecip(out, in_, reads=(), writes=()):
            S.op("dve", lambda e: e.reciprocal(out, in_), reads, writes)

        def vscan(out, d0, d1, reads=(), writes=()):
            S.op("dve", lambda e: e.tensor_tensor_scan(out, d0, d1, 0.0, ALU.mult, ALU.add), reads, writes)

        def ld(out, in_, writes, reads=()):
            S.dma("sp", out, in_, reads=reads, writes=writes)

        def ldc(out, in_, writes, reads=()):
            S.dma("pool", out, in_, reads=reads, writes=writes, max_dma_last_dim=4096)

        def st(out, in_, reads, writes):
            S.dma("sp", out, in_, reads=reads, writes=writes)

        ring = {"wr": 0, "pg": 0, "pw": 0, "sw": 0}

        def wload(w_ap, nk):
            i = ring["wr"]
            ring["wr"] = (i + 1) % NWR
            ldc(WR[i][:, 0:nk, :].rearrange("p k c -> p (k c)"), w_ap, writes=[("wr", i)])
            return WR[i], ("wr", i)

        def nextpg():
            i = ring["pg"]
            ring["pg"] = (i + 1) % 4
            return PG[i], ("pg", i)

        def gemm(w_ap, nk, rhs, rkey, Tn):
            buf, wkey = wload(w_ap, nk)
            ps, pk = nextpg()
            for kc in range(nk):
                mm(ps[:, :Tn], buf[:, kc, :], rhs(kc), start=(kc == 0), stop=(kc == nk - 1),
                   reads=[wkey, rkey(kc)], writes=[pk])
            return ps, pk

        ld(vec[:].rearrange("p v k -> p (v k)"), vec_d, writes=["vec"])
        ld(gn[:], gn_d, writes=["gn"])
        ld(cvp[:].rearrange("p l c f -> p (l c f)"), cvp_d, writes=["cvp"])
        ld(onesD[:], c_onesD, writes=["onesD"])
        ld(ones128[:], c_ones128, writes=["ones128"])
        ld(bones[:], c_bones, writes=["bones"])
        ldc(ident[:], c_ident, writes=["ident"])
        ld(masks[:].rearrange("p a c -> p (a c)"), c_mask, writes=["masks"])
        ld(cm[:], c_cm, writes=["cm"])
        ld(cstS[:].rearrange("p l c r j -> p l (c r j)"), st_cv_s.rearrange("l p f -> p l f"), writes=["cstS"])
        ld(xlS[:].rearrange("p r k -> p (r k)"), st_sh_s, writes=["xlS"])
        vmemset(cstP[:].rearrange("p l c r j -> p (l c r j)"), 0.0, writes=["cstP"])
        vmemset(xlP[:].rearrange("p r k -> p (r k)"), 0.0, writes=["xlP"])
        tt(dvec[:, 0, :], vec[:, V_LB0, :], vec[:, V_LB1, :], ALU.subtract, reads=["vec"], writes=["dvec"])
        act(dvec[:, 0, :], dvec[:, 0, :], AF.Sigmoid, reads=["dvec"], writes=["dvec"])
        ts(dvec[:, 1, :], dvec[:, 0, :], -1.0, 1.0, ALU.mult, ALU.add, reads=["dvec"], writes=["dvec"])
        ts(dvec[:, 2, :], vec[:, V_KA, :], -1.0, 1.0, ALU.mult, ALU.add, reads=["vec"], writes=["dvec"])

        mI = {64: masks[:, 0, :], 32: masks[0:64, 3, 0:32]}
        mS = {64: masks[:, 1, :], 32: masks[0:64, 4, 0:32]}
        mL = {64: masks[0:64, 2, :], 32: masks[0:32, 5, 0:32]}

        XBf = XB[:].rearrange("p (k t) -> p k t", t=T)
        OBf = OB[:].rearrange("p (k t) -> p k t", t=T)
        XB4 = XB[:].rearrange("p (a k t) -> p a k t", a=4, t=TSUB)
        OB2 = OB[:].rearrange("p (a k t) -> p a k t", a=2, t=TSUB)

        def rmsnorm(src, gidx, dst, Tn, skey, dkey, extra=None):
            ps, pk = nextpg()
            for kc in range(KC):
                tq = TF[kc % 2]
                act(tq[:, :Tn], src(kc), AF.Square, reads=[skey(kc)], writes=[("tf", kc % 2)])
                mm(ps[:, :Tn], onesD[:], tq[:, :Tn], start=(kc == 0), stop=(kc == KC - 1),
                   reads=["onesD", ("tf", kc % 2)], writes=[pk])
            rs = TF[2]
            act(rs[:, :Tn], ps[:, :Tn], AF.Sqrt, reads=[pk], writes=[pk, ("tf", 2)], bias=RMS_EPS)
            vrecip(rs[:, :Tn], rs[:, :Tn], reads=[("tf", 2)], writes=[("tf", 2)])
            for kc in range(KC):
                stt(dst(kc), src(kc), vec[:, gidx, kc:kc + 1], rs[:, :Tn], ALU.mult, ALU.mult,
                    reads=[skey(kc), "vec", ("tf", 2)], writes=[dkey(kc)])
            return rs, ("tf", 2)

        def hgrn_phase(td):
            Tn, R, L, C = td["T"], td["R"], td["L"], td["C"]
            nch = L // C
            cmv = cm[:, 0:T] if C == 64 else cm[:, T:T + TSUB]
            rmsnorm(lambda kc: h[:, kc, :Tn], V_NMIX0, lambda kc: XBf[:, kc, :Tn], Tn,
                    lambda kc: ("h", kc), lambda kc: ("xn", kc))
            rhs = lambda kc: XBf[:, kc, :Tn]
            rkey = lambda kc: ("xn", kc)
            for hd in range(32):
                par = hd % 2
                t0, t1, t2 = TF[3 + par * 3], TF[4 + par * 3], TF[5 + par * 3]
                k0, k1, k2 = ("tf", 3 + par * 3), ("tf", 4 + par * 3), ("tf", 5 + par * 3)
                t3, k3 = TF[9], ("tf", 9)
                qT, kT, vT = TB[par * 3], TB[par * 3 + 1], TB[par * 3 + 2]
                qk, kk_, vk = ("tb", par * 3), ("tb", par * 3 + 1), ("tb", par * 3 + 2)
                ps, pk = gemm(w_in[32 + hd], KC, rhs, rkey, Tn)
                act(t0[:, :Tn], ps[:, :Tn], AF.Sigmoid, reads=[], writes=[pk, k0])
                ts(t0[:, :Tn], t0[:, :Tn], dvec[:, 1, hd:hd + 1], dvec[:, 0, hd:hd + 1], ALU.mult, ALU.add,
                   reads=["dvec"], writes=[k0])
                act(t1[:, :Tn], t0[:, :Tn], AF.Ln, reads=[k0], writes=[k1])
                ts(t0[:, :Tn], t0[:, :Tn], -1.0, 1.0, ALU.mult, ALU.add, reads=[k1], writes=[k0])
                vscan(t2[:, :Tn], cmv[:, :Tn], t1[:, :Tn], reads=["cm", k1], writes=[k2])
                act(t1[:, :Tn], t2[:, :Tn], AF.Exp, reads=[k2], writes=[k1])
                act(t2[:, :Tn], t2[:, :Tn], AF.Exp, reads=[], writes=[k2], scale=-1.0)
                tt(kT[:, :Tn], t0[:, :Tn], t2[:, :Tn], ALU.mult, reads=[k0, k2], writes=[kk_])
                ps, pk = gemm(w_in[hd], KC, rhs, rkey, Tn)
                act(t0[:, :Tn], ps[:, :Tn], AF.Silu, reads=[], writes=[pk, k0])
                tt(qT[:, :Tn], t0[:, :Tn], t1[:, :Tn], ALU.mult, reads=[k0, k1], writes=[qk])
                ps, pk = gemm(w_in[64 + hd], KC, rhs, rkey, Tn)
                act(vT[:, :Tn], ps[:, :Tn], AF.Copy, reads=[], writes=[pk, vk])
                ps, pk = gemm(w_in[96 + hd], KC, rhs, rkey, Tn)
                act(t3[:, :Tn], ps[:, :Tn], AF.Sigmoid, reads=[], writes=[pk, k3])
                if td["kind"] == "p":
                    if td["idx"] == 0:
                        vmemset(Sst[:, 0, :], 0.0, writes=["Sst"])
                    else:
                        ld(Sst[:, 0, :], o_hg_p[hd], reads=[("o_hg_p", hd)], writes=["Sst"])
                else:
                    ld(Sst[:, :, :].rearrange("p r v -> p (r v)"), st_hg_s[hd], writes=["Sst"])
                act(Sbf[:, 0:R, :], Sst[:, 0:R, :], AF.Copy, reads=["Sst"], writes=["Sbf"])
                for r in range(R):
                    for c in range(nch):
                        c0 = r * L + c * C
                        cols = slice(c0, c0 + C)
                        vkb, vkk = VK[c % 2], ("VK", c % 2)
                        atb, atk = ATt[c % 2], ("AT", c % 2)
                        tr(pT_[0:C, 0:128], vT[:, cols], ident[:], reads=[vk, "ident"], writes=["pT"])
                        tr(pT_[0:C, 128:256], kT[:, cols], ident[:], reads=[kk_, "ident"], writes=["pT"])
                        act(vkb[0:C, :], pT_[0:C, 0:256], AF.Copy, reads=[], writes=["pT", vkk])
                        mm(pS[0:C, 0:C], kT[:, cols], qT[:, cols], reads=[kk_, qk], writes=["pS"])
                        tt(atb[0:C, 0:C], pS[0:C, 0:C], mI[C][0:C, :], ALU.mult, reads=["masks"], writes=["pS", atk])
                        mm(pO[:, cols], vkb[0:C, 0:128], atb[0:C, 0:C], start=True, stop=False,
                           reads=[vkk, atk], writes=["pO"])
                        mm(pO[:, cols], Sbf[:, r, :], qT[:, cols], start=False, stop=True,
                           reads=["Sbf", qk], writes=["pO"])
                        mm(pS2[:, 0:128], vkb[0:C, 128:256], vkb[0:C, 0:128], reads=[vkk], writes=["pS2"])
                        tt(Sst[:, r, :], pS2[:, 0:128], Sst[:, r, :], ALU.add, reads=[], writes=["pS2", "Sst"])
                        ts(Sst[:, r, :], Sst[:, r, :], t1[:, c0 + C - 1:c0 + C], None, ALU.mult,
                           reads=[k1], writes=["Sst"])
                        act(Sbf[:, r, :], Sst[:, r, :], AF.Copy, reads=["Sst"], writes=["Sbf"])
                if td["kind"] == "p":
                    st(o_hg_p[hd], Sst[:, 0, :], reads=["Sst"], writes=[("o_hg_p", hd)])
                else:
                    st(o_hg_s[hd], Sst[:, :, :].rearrange("p r v -> p (r v)"), reads=["Sst"], writes=[("o_hg_s", hd)])
                act(t0[:, :Tn], pO[:, :Tn], AF.Square, reads=[], writes=["pO", k0])
                ps, pk = nextpg()
                mm(ps[:, :Tn], ones128[:], t0[:, :Tn], reads=["ones128", k0], writes=[pk])
                act(t0[:, :Tn], ps[:, :Tn], AF.Sqrt, reads=[], writes=[pk, k0], bias=RMS_EPS)
                vrecip(t0[:, :Tn], t0[:, :Tn], reads=[], writes=[k0])
                stt(t2[:, :Tn], pO[:, :Tn], gn[:, 0:1], t0[:, :Tn], ALU.mult, ALU.mult,
                    reads=["gn", k0], writes=["pO", k2])
                tt(OBf[:, hd % 16, :Tn], t2[:, :Tn], t3[:, :Tn], ALU.mult, reads=[k2, k3], writes=[("ob", hd % 16)])
                if hd % 16 == 15:
                    gi = hd // 16
                    for m in range(32):
                        ps, pk = gemm(w_o0[gi, m], 16, lambda kc: OBf[:, kc, :Tn], lambda kc: ("ob", kc), Tn)
                        tt(h[:, m, :Tn], ps[:, :Tn], h[:, m, :Tn], ALU.add, reads=[], writes=[pk, ("h", m)])

        def ffn_phase(td, l):
            Tn, R, L = td["T"], td["R"], td["L"]
            cst = cstP if td["kind"] == "p" else cstS
            ckey = "cstP" if td["kind"] == "p" else "cstS"
            rmsnorm(lambda kc: h[:, kc, :Tn], V_NFFN0 + l, lambda kc: XBf[:, kc, :Tn], Tn,
                    lambda kc: ("h", kc), lambda kc: ("xn", kc))
            rhs = lambda kc: XBf[:, kc, :Tn]
            rkey = lambda kc: ("xn", kc)
            for gi, (ca, cb) in enumerate(FGROUPS):
                for c in range(ca, cb):
                    par = c % 2
                    hg = HGP[par][:, 0:R * (L + 2)].rearrange("p (r l) -> p r l", l=L + 2)
                    hk = ("hgp", par)
                    acc, ak = TF[3 + par * 2], ("tf", 3 + par * 2)
                    ge, gk = TF[4 + par * 2], ("tf", 4 + par * 2)
                    accv = acc[:, :Tn].rearrange("p (r l) -> p r l", l=L)
                    psA, pkA = gemm(w_up[l, c], KC, rhs, rkey, Tn)
                    act(hg[:, :, 2:L + 2], psA[:, :Tn].rearrange("p (r l) -> p r l", l=L), AF.Copy,
                        reads=[], writes=[pkA, hk])
                    vcopy(hg[:, :, 0:2], cst[:, l, c, 0:R, :], reads=[ckey], writes=[hk])
                    ts(accv, hg[:, :, 0:L], cvp[:, l, c, 0:1], cvp[:, l, c, 3:4], ALU.mult, ALU.add,
                       reads=["cvp", hk], writes=[ak])
                    stt(accv, hg[:, :, 1:L + 1], cvp[:, l, c, 1:2], accv, ALU.mult, ALU.add,
                        reads=["cvp", hk], writes=[ak])
                    stt(accv, hg[:, :, 2:L + 2], cvp[:, l, c, 2:3], accv, ALU.mult, ALU.add,
                        reads=["cvp", hk], writes=[ak])
                    vcopy(cst[:, l, c, 0:R, :], hg[:, :, L:L + 2], reads=[hk], writes=[ckey])
                    act(ge[:, :Tn], acc[:, :Tn], AF.Gelu, reads=[ak], writes=[gk])
                    psB, pkB = gemm(w_up[l, FCH + c], KC, rhs, rkey, Tn)
                    tt(OBf[:, c - ca, :Tn], psB[:, :Tn], ge[:, :Tn], ALU.mult, reads=[gk], writes=[pkB, ("ob", c - ca)])
                G = cb - ca
                for m in range(32):
                    w_ap = w_dnA[l, gi, m] if gi < 5 else w_dnB[l, m]
                    ps, pk = gemm(w_ap, G, lambda kc: OBf[:, kc, :Tn], lambda kc: ("ob", kc), Tn)
                    tt(h[:, m, :Tn], ps[:, :Tn], h[:, m, :Tn], ALU.add, reads=[], writes=[pk, ("h", m)])

        def ple_phase(td, l):
            Tn = td["T"]
            rmsnorm(lambda kc: h[:, kc, :Tn], V_NPLE0 + l, lambda kc: XBf[:, kc, :Tn], Tn,
                    lambda kc: ("h", kc), lambda kc: ("xn", kc))
            src = pT_p[l, td["idx"]] if td["kind"] == "p" else pT_s[l]
            ldc(pT[:, :, :Tn], src.rearrange("p (k t) -> p k t", k=2), writes=["pTin"])
            for m in range(32):
                i = ring["pw"]
                ring["pw"] = 1 - i
                ldc(PW[i][:].rearrange("p k c -> p (k c)"), w_pp[l, m], writes=[("pw", i)])
                psA, pkA = gemm(w_pg[l, m], KC, lambda kc: XBf[:, kc, :Tn], lambda kc: ("xn", kc), Tn)
                psB, pkB = nextpg()
                for kc in range(2):
                    mm(psB[:, :Tn], PW[i][:, kc, :], pT[:, kc, :Tn], start=(kc == 0), stop=(kc == 1),
                       reads=[("pw", i), "pTin"], writes=[pkB])
                par = m % 2
                sg, sk = TF[3 + par], ("tf", 3 + par)
                act(sg[:, :Tn], psA[:, :Tn], AF.Sigmoid, reads=[], writes=[pkA, sk])
                tt(sg[:, :Tn], psB[:, :Tn], sg[:, :Tn], ALU.mult, reads=[], writes=[pkB, sk])
                tt(h[:, m, :Tn], sg[:, :Tn], h[:, m, :Tn], ALU.add, reads=[sk], writes=[("h", m)])

        def rwkv_sub(td, c_off):
            R, L, C = td["R"], td["L"], td["C"]
            if td["kind"] == "p":
                R, L = 1, TSUB
            Ts = TSUB
            nch = L // C
            nct = R * nch
            cmv = cm[:, 0:TSUB] if C == 64 else cm[:, T:T + TSUB]
            XN, XX, KA, VA = XB4[:, 0], XB4[:, 1], XB4[:, 2], XB4[:, 3]
            XV, YG = OB2[:, 0], OB2[:, 1]
            xprev = xlP if td["kind"] == "p" else xlS
            xpk = "xlP" if td["kind"] == "p" else "xlS"
            lvl = 0
            while (1 << lvl) < C:
                lvl += 1
            def sub(i):
                return TF[3 + i // 4][:, (i % 4) * TSUB:(i % 4 + 1) * TSUB], ("tfs", i)
            S.barrier()
            rs, rsk = rmsnorm(lambda kc: h[:, kc, c_off:c_off + Ts], V_NMIX1, lambda kc: XN[:, kc, :], Ts,
                              lambda kc: ("h", kc), lambda kc: ("rxn", kc))
            for r in range(R):
                col = c_off + r * L + L - 1
                stt(xlN[:, r, :], h[:, :, col], rs[:, r * L + L - 1:r * L + L], vec[:, V_NMIX1, :],
                    ALU.mult, ALU.mult, reads=[("h", kc) for kc in range(KC)] + [rsk, "vec"], writes=["xlN"])
            XN4 = XN.rearrange("p k (r l) -> p k r l", l=L)
            XX4 = XX.rearrange("p k (r l) -> p k r l", l=L)
            for kc in range(KC):
                tt(XX4[:, kc, :, 1:L], XN4[:, kc, :, 0:L - 1], XN4[:, kc, :, 1:L], ALU.subtract,
                   reads=[("rxn", kc)], writes=[("rxx", kc)])
            for r in range(R):
                tt(XX4[:, :, r, 0], xprev[:, r, :], XN4[:, :, r, 0], ALU.subtract,
                   reads=[xpk] + [("rxn", kc) for kc in range(KC)], writes=[("rxx", kc) for kc in range(KC)])
            if td["kind"] == "p":
                vcopy(xlP[:, 0, :], xlN[:, 0, :], reads=["xlN"], writes=["xlP"])
            def variant(mu_idx):
                for kc in range(KC):
                    stt(XV[:, kc, :], XX[:, kc, :], vec[:, mu_idx, kc:kc + 1], XN[:, kc, :], ALU.mult, ALU.add,
                        reads=[("rxx", kc), ("rxn", kc), "vec"], writes=[("rxv", kc)])
            vrhs = lambda kc: XV[:, kc, :]
            vkey = lambda kc: ("rxv", kc)
            variant(V_MUK)
            for m in range(32):
                ps, pk = gemm(w_rkv[32 + m], KC, vrhs, vkey, Ts)
                act(KA[:, m, :], ps[:, :Ts], AF.Copy, reads=[], writes=[pk, ("rka", m)])
            variant(V_MUV)
            for m in range(32):
                ps, pk = gemm(w_rkv[64 + m], KC, vrhs, vkey, Ts)
                act(VA[:, m, :], ps[:, :Ts], AF.Copy, reads=[], writes=[pk, ("rva", m)])
            variant(V_MUW)
            ps, pk = gemm(w_l1[0], KC, vrhs, vkey, Ts)
            act(hid[:, 0, :], ps[:, :Ts], AF.Tanh, reads=[], writes=[pk, "hid"])
            variant(V_MUA)
            ps, pk = gemm(w_l1[1], KC, vrhs, vkey, Ts)
            act(hid[:, 1, :], ps[:, :Ts], AF.Copy, reads=[], writes=[pk, "hid"])
            variant(V_MUG)
            for j in range(4):
                ps, pk = gemm(w_l1[2 + j], KC, vrhs, vkey, Ts)
                act(hid[:, 2 + j, :], ps[:, :Ts], AF.Sigmoid, reads=[], writes=[pk, "hid"])
            variant(V_MUR)
            BKT3 = BKT[:, 0:nct * 2 * C].rearrange("p (n c) -> p n c", c=2 * C)
            VT3 = VT[:, 0:nct * 2 * C].rearrange("p (n c) -> p n c", c=2 * C)
            ch3 = lambda ap: ap.rearrange("p (n c) -> p n c", c=C)
            for m in range(32):
                i = ring["sw"]
                ring["sw"] = 1 - i
                sw, swk = SW[i], ("sw", i)
                ldc(sw[:].rearrange("p k c -> p (k c)"), w_sw[m], writes=[swk])
                tmp = [sub(j) for j in range(20)]
                (r_, rk_), (lw, lwk), (G_, Gk), (eG, eGk), (enG, enGk), (eGm, eGmk), (a_, ak_), (kkr, kkrk), \
                    (sq, sqk), (kk, kkk), (kh, khk), (tq, tqk), (g_, gk_), (bon, bonk), (y_, yk), (yc, yck) = tmp[:16]
                AT_, ATk = TB[0][:, 0:Ts], ("tbs", 0)
                RT_, RTk = TB[0][:, Ts:2 * Ts], ("tbs", 1)
                ps, pk = gemm(w_rkv[m], KC, vrhs, vkey, Ts)
                act(r_, ps[:, :Ts], AF.Copy, reads=[], writes=[pk, rk_])
                ps, pk = nextpg()
                mm(ps[:, :Ts], sw[:, 0, :], hid[:, 0, :], reads=[swk, "hid"], writes=[pk])
                act(lw, ps[:, :Ts], AF.Sigmoid, reads=["vec"], writes=[pk, lwk], bias=vec[:, V_W0, m:m + 1])
                ts(lw, lw, -WDEC, None, ALU.mult, reads=[], writes=[lwk])
                ps, pk = nextpg()
                mm(ps[:, :Ts], sw[:, 1, :], hid[:, 1, :], reads=[swk, "hid"], writes=[pk])
                act(a_, ps[:, :Ts], AF.Sigmoid, reads=["vec"], writes=[pk, ak_], bias=vec[:, V_A0, m:m + 1])
                ps, pk = nextpg()
                for j in range(4):
                    mm(ps[:, :Ts], sw[:, 2 + j, :], hid[:, 2 + j, :], start=(j == 0), stop=(j == 3),
                       reads=[swk, "hid"], writes=[pk])
                act(g_, ps[:, :Ts], AF.Copy, reads=[], writes=[pk, gk_])
                vscan(G_, cmv, lw, reads=["cm", lwk], writes=[Gk])
                act(eG, G_, AF.Exp, reads=[Gk], writes=[eGk])
                act(enG, G_, AF.Exp, reads=[Gk], writes=[enGk], scale=-1.0)
                tt(tq, G_, lw, ALU.subtract, reads=[Gk, lwk], writes=[tqk])
                act(eGm, tq, AF.Exp, reads=[tqk], writes=[eGmk])
                ts(kkr, KA[:, m, :], vec[:, V_KK, m:m + 1], None, ALU.mult, reads=[("rka", m), "vec"], writes=[kkrk])
                act(sq, kkr, AF.Square, reads=[kkrk], writes=[sqk])
                ps, pk = nextpg()
                mm(ps[:, :Ts], bones[:], sq, reads=["bones", sqk], writes=[pk])
                ts(sq, ps[:, :Ts], 64.0, 1e-24, ALU.mult, ALU.max, reads=[], writes=[pk, sqk])
                act(sq, sq, AF.Sqrt, reads=[], writes=[sqk])
                vrecip(sq, sq, reads=[], writes=[sqk])
                tt(kk, kkr, sq, ALU.mult, reads=[kkrk, sqk], writes=[kkk])
                ts(tq, a_, vec[:, V_KA, m:m + 1], dvec[:, 2, m:m + 1], ALU.mult, ALU.add,
                   reads=[ak_, "vec", "dvec"], writes=[tqk])
                tt(kh, KA[:, m, :], tq, ALU.mult, reads=[("rka", m), tqk], writes=[khk])
                tt(AT_, kk, eGm, ALU.mult, reads=[kkk, eGmk], writes=[ATk])
                tt(tq, kk, a_, ALU.mult, reads=[kkk, ak_], writes=[tqk])
                tt(BKT3[:, :, 0:C], ch3(tq), ch3(enG), ALU.mult, reads=[tqk, enGk], writes=["BKT"])
                tt(BKT3[:, :, C:2 * C], ch3(kh), ch3(enG), ALU.mult, reads=[khk, enGk], writes=["BKT"])
                tt(RT_, r_, eG, ALU.mult, reads=[rk_, eGk], writes=[RTk])
                vcopy(VT3[:, :, C:2 * C], ch3(VA[:, m, :]), reads=[("rva", m)], writes=["VT"])
                vcopy(VT3[:, :, 0:C], ch3(VA[:, m, :]), reads=[("rva", m)], writes=["VT"])
                tt(tq, r_, kh, ALU.mult, reads=[rk_, khk], writes=[tqk])
                ts(tq, tq, vec[:, V_RK, m:m + 1], None, ALU.mult, reads=["vec"], writes=[tqk])
                ps, pk = nextpg()
                mm(ps[:, :Ts], bones[:], tq, reads=["bones", tqk], writes=[pk])
                stt(bon, ps[:, :Ts], 64.0, VA[:, m, :], ALU.mult, ALU.mult, reads=[("rva", m)], writes=[pk, bonk])
                if td["kind"] == "p":
                    if td["idx"] == 0 and c_off == 0:
                        vmemset(Hst[:, 0, :], 0.0, writes=["Hst"])
                    else:
                        ld(Hst[:, 0, :], o_rw_p[m], reads=[("o_rw_p", m)], writes=["Hst"])
                else:
                    ld(Hst[:, :, :].rearrange("p r v -> p (r v)"), st_rw_s[m], writes=["Hst"])
                act(Hbf[:, 0:R, :], Hst[:, 0:R, :], AF.Copy, reads=["Hst"], writes=["Hbf"])
                for r in range(R):
                    for c in range(nch):
                        n = r * nch + c
                        c0 = r * L + c * C
                        cols = slice(c0, c0 + C)
                        for hh in range(2):
                            P0 = hh * 64
                            PS = slice(P0, P0 + 64)
                            mm(pS[0:2 * C, 0:C], BKT3[PS, n, :], AT_[PS, cols], reads=["BKT", ATk], writes=["pS"])
                            mm(pS[0:C, 64:64 + C], AT_[PS, cols], BKT3[PS, n, 0:C], reads=["BKT", ATk], writes=["pS"])
                            mm(pS2[0:2 * C, 0:C], BKT3[PS, n, :], RT_[PS, cols], reads=["BKT", RTk], writes=["pS2"])
                            tt(NM[0:C, 0:C], pS[0:C, 0:C], mS[C][0:C, :], ALU.mult, reads=["masks"], writes=["pS", "NM"])
                            tt(NMb[C:2 * C, 0:C], pS[C:2 * C, 0:C], mS[C][C:2 * C, :], ALU.mult,
                               reads=["masks"], writes=["pS", "NMb"])
                            tt(N0[0][0:C, 0:C], pS[0:C, 64:64 + C], mL[C], ALU.mult, reads=["masks"], writes=["pS", ("N0", 0)])
                            tt(MR[0:2 * C, 0:C], pS2[0:2 * C, 0:C], mI[C], ALU.mult, reads=["masks"], writes=["pS2", "MR"])
                            tr(pT_[0:2 * C, 0:64], BKT3[PS, n, :], ident[PS, PS], reads=["BKT", "ident"], writes=["pT"])
                            tr(pT_[0:2 * C, 64:128], VT3[PS, n, :], ident[PS, PS], reads=["VT", "ident"], writes=["pT"])
                            act(BK[0:2 * C, :], pT_[0:2 * C, 0:64], AF.Copy, reads=[], writes=["pT", "BK"])
                            act(UV[C:2 * C, :], pT_[C:2 * C, 64:128], AF.Copy, reads=[], writes=["pT", "UV"])
                            mm(pS2[0:C, 64:128], AT_[PS, cols], Hbf[PS, r, :], start=True, stop=False,
                               reads=[ATk, "Hbf"], writes=["pS2"])
                            mm(pS2[0:C, 64:128], NMb[C:2 * C, 0:C], UV[C:2 * C, :], start=False, stop=True,
                               reads=["NMb", "UV"], writes=["pS2"])
                            ts(Wt[0][0:C, :], pS2[0:C, 64:128], -1.0, None, ALU.mult, reads=[], writes=["pS2", ("Wt", 0)])
                            wi = 0
                            ni = 0
                            curN, curNk = N0[0], ("N0", 0)
                            curT, curTk = NM, "NM"
                            for k in range(lvl):
                                last = (k == lvl - 1)
                                mm(pS[0:C, 0:64], curT[0:C, 0:C], Wt[wi][0:C, :], reads=[curTk, ("Wt", wi)], writes=["pS"])
                                if not last:
                                    mm(pS2[0:C, 0:C], curT[0:C, 0:C], curN[0:C, 0:C], reads=[curTk, curNk], writes=["pS2"])
                                    mm(pS2[0:C, 64:64 + C], curN[0:C, 0:C], curT[0:C, 0:C], reads=[curTk, curNk], writes=["pS2"])
                                op = ALU.subtract if k == 0 else ALU.add
                                if last:
                                    tt(UV[0:C, :], Wt[wi][0:C, :], pS[0:C, 0:64], op, reads=[("Wt", wi)], writes=["pS", "UV"])
                                else:
                                    tt(Wt[1 - wi][0:C, :], Wt[wi][0:C, :], pS[0:C, 0:64], op,
                                       reads=[("Wt", wi)], writes=["pS", ("Wt", 1 - wi)])
                                    wi = 1 - wi
                                    nn = 1 - ni if k > 0 else 1
                                    act(N0[nn][0:C, 0:C], pS2[0:C, 0:C], AF.Copy, reads=[], writes=["pS2", ("N0", nn)])
                                    act(NT_[nn][0:C, 0:C], pS2[0:C, 64:64 + C], AF.Copy, reads=[], writes=["pS2", ("NT", nn)])
                                    curN, curNk = N0[nn], ("N0", nn)
                                    curT, curTk = NT_[nn], ("NT", nn)
                                    ni = nn
                            mm(pO[PS, cols], UV[0:2 * C, :], MR[0:2 * C, 0:C], start=True, stop=False,
                               reads=["UV", "MR"], writes=["pO"])
                            mm(pO[PS, cols], Hbf[PS, r, :], RT_[PS, cols], start=False, stop=True,
                               reads=["Hbf", RTk], writes=["pO"])
                            mm(pS[PS, 128:192], BK[0:2 * C, :], UV[0:2 * C, :], reads=["BK", "UV"], writes=["pS"])
                            tt(Hst[PS, r, :], pS[PS, 128:192], Hst[PS, r, :], ALU.add, reads=[], writes=["pS", "Hst"])
                            ts(Hst[PS, r, :], Hst[PS, r, :], eG[PS, c0 + C - 1:c0 + C], None, ALU.mult,
                               reads=[eGk], writes=["Hst"])
                            act(Hbf[PS, r, :], Hst[PS, r, :], AF.Copy, reads=["Hst"], writes=["Hbf"])
                if td["kind"] == "p":
                    st(o_rw_p[m], Hst[:, 0, :], reads=["Hst"], writes=[("o_rw_p", m)])
                else:
                    st(o_rw_s[m], Hst[:, :, :].rearrange("p r v -> p (r v)"), reads=["Hst"], writes=[("o_rw_s", m)])
                act(y_, pO[:, :Ts], AF.Copy, reads=[], writes=["pO", yk])
                ps, pk = nextpg()
                mm(ps[:, :Ts], bones[:], y_, reads=["bones", yk], writes=[pk])
                tt(yc, y_, ps[:, :Ts], ALU.subtract, reads=[yk], writes=[pk, yck])
                act(sq, yc, AF.Square, reads=[yck], writes=[sqk])
                ps, pk = nextpg()
                mm(ps[:, :Ts], bones[:], sq, reads=["bones", sqk], writes=[pk])
                act(sq, ps[:, :Ts], AF.Sqrt, reads=[], writes=[pk, sqk], bias=GN_EPS)
                vrecip(sq, sq, reads=[], writes=[sqk])
                tt(yc, yc, sq, ALU.mult, reads=[sqk], writes=[yck])
                ts(yc, yc, vec[:, V_LNW, m:m + 1], vec[:, V_LNB, m:m + 1], ALU.mult, ALU.add, reads=["vec"], writes=[yck])
                tt(yc, yc, bon, ALU.add, reads=[bonk], writes=[yck])
                tt(YG[:, m, :], yc, g_, ALU.mult, reads=[yck, gk_], writes=[("ryg", m)])
            for mo in range(32):
                ps, pk = gemm(w_o1[mo], KC, lambda kc: YG[:, kc, :], lambda kc: ("ryg", kc), Ts)
                tt(h[:, mo, c_off:c_off + Ts], ps[:, :Ts], h[:, mo, c_off:c_off + Ts], ALU.add,
                   reads=[], writes=[pk, ("h", mo)])
            S.barrier()

        def rwkv_phase(td):
            nsub = td["T"] // TSUB if td["kind"] == "p" else 1
            for s in range(nsub):
                rwkv_sub(td, s * TSUB)

        def final_phase(td):
            Tn = td["T"]
            ps, pk = nextpg()
            for kc in range(KC):
                tq = TF[kc % 2]
                act(tq[:, :Tn], h[:, kc, :Tn], AF.Square, reads=[("h", kc)], writes=[("tf", kc % 2)])
                mm(ps[:, :Tn], onesD[:], tq[:, :Tn], start=(kc == 0), stop=(kc == KC - 1),
                   reads=["onesD", ("tf", kc % 2)], writes=[pk])
            rs = TF[2]
            act(rs[:, :Tn], ps[:, :Tn], AF.Sqrt, reads=[], writes=[pk, ("tf", 2)], bias=RMS_EPS)
            vrecip(rs[:, :Tn], rs[:, :Tn], reads=[], writes=[("tf", 2)])
            for kc in range(KC):
                stt(h[:, kc, :Tn], h[:, kc, :Tn], vec[:, V_NFIN, kc:kc + 1], rs[:, :Tn], ALU.mult, ALU.mult,
                    reads=["vec", ("tf", 2)], writes=[("h", kc)])
            dst = yT_p[td["idx"]] if td["kind"] == "p" else yT_s
            dstv = dst.rearrange("p (k t) -> p k t", k=KC)
            for k8 in range(0, KC, 8):
                st(dstv[:, k8:k8 + 8, :], h[:, k8:k8 + 8, :Tn], reads=[("h", kc) for kc in range(k8, k8 + 8)], writes=["yout"])

        tiles = [dict(kind="p", idx=i, T=T, R=1, L=T, C=64) for i in range(NPT)]
        tiles.append(dict(kind="s", idx=0, T=TSUB, R=RS, L=LS, C=32))
        phases = ["hgrn", "ffn0", "ple0", "rwkv", "ffn1", "ple1"]
        for td in tiles:
            Tn = td["T"]
            src = xT_p[td["idx"]] if td["kind"] == "p" else xT_s
            srcv = src.rearrange("p (k t) -> p k t", k=KC)
            for k8 in range(0, KC, 8):
                ld(h[:, k8:k8 + 8, :Tn], srcv[:, k8:k8 + 8, :], writes=[("h", kc) for kc in range(k8, k8 + 8)])
            for ph in phases:
                if ph == "hgrn":
                    hgrn_phase(td)
                elif ph == "rwkv":
                    rwkv_phase(td)
                elif ph.startswith("ffn"):
                    ffn_phase(td, int(ph[3]))
                elif ph.startswith("ple"):
                    ple_phase(td, int(ph[3]))
                if stop_after == ph:
                    break
            final_phase(td)
            last_p = td["kind"] == "p" and td["idx"] == NPT - 1
            if last_p:
                st(o_sh_p, xlP[:, 0, :], reads=["xlP"], writes=["o_sh_p"])
                st(o_cv_p.rearrange("l p f -> p l f"), cstP[:].rearrange("p l c r j -> p l (c r j)"),
                   reads=["cstP"], writes=["o_cv_p"])
            if td["kind"] == "s":
                st(o_sh_s, xlN[:].rearrange("p r k -> p (r k)"), reads=["xlN"], writes=["o_sh_s"])
                st(o_cv_s.rearrange("l p f -> p l f"), cstS[:].rearrange("p l c r j -> p l (c r j)"),
                   reads=["cstS"], writes=["o_cv_s"])
        S.finish()
        print("ops:", S.nops, {k: len(v) for k, v in S.ops.items()}, "sbuf left", nc.sbuf_bytes_remaining)
    return nc


def _tile_w(W):
    K, N = W.shape
    kc, nm = K // 128, N // 128
    return np.ascontiguousarray(W.reshape(kc, 128, nm, 128).transpose(2, 1, 0, 3)).reshape(nm, 128, kc * 128)


def _fm(x):
    t, f = x.shape
    return np.ascontiguousarray(x.reshape(t, f // 128, 128).transpose(2, 1, 0)).reshape(128, (f // 128) * t)


def _vecfm(v):
    return np.ascontiguousarray(np.asarray(v, np.float32).reshape(32, 128).T)


def _consts():
    c = {}
    c["c_onesD"] = np.full((128, 128), 1.0 / D, np.float32)
    c["c_ones128"] = np.full((128, 128), 1.0 / 128, np.float32)
    b = np.zeros((128, 128), np.float32)
    b[:64, :64] = 1.0 / 64
    b[64:, 64:] = 1.0 / 64
    c["c_bones"] = b
    c["c_ident"] = np.eye(128, dtype=np.float32)
    m = np.zeros((128, 6, 64), np.float32)
    p = np.arange(128)[:, None]
    t = np.arange(64)[None, :]
    m[:, 0, :] = ((p % 64) <= t)
    m[:, 1, :] = ((p % 64) < t)
    m[:, 2, :] = (p > t) & (p < 64)
    m[:, 3, :] = ((p % 32) <= t) & (t < 32) & (p < 64)
    m[:, 4, :] = ((p % 32) < t) & (t < 32) & (p < 64)
    m[:, 5, :] = (p > t) & (p < 32) & (t < 32)
    c["c_mask"] = m.reshape(128, 6 * 64)
    cmk = np.ones((128, T + TSUB), np.float32)
    cmk[:, 0:T:64] = 0.0
    cmk[:, T::32] = 0.0
    c["c_cm"] = cmk
    return c


def prep_shared(inp):
    f = lambda k: np.asarray(inp[k], np.float32)
    sh = {}
    sh["w_in"] = _tile_w(f("hg_w_in")[0])
    wo = f("hg_w_o")[0]
    sh["w_o0"] = np.stack([_tile_w(wo[g * 2048:(g + 1) * 2048]) for g in range(2)])
    sh["w_rkv"] = _tile_w(f("rw_w_rkv")[0])
    g1 = np.zeros((D, 512), np.float32)
    g1[:, :480] = f("rw_g1")[0]
    sh["w_l1"] = np.concatenate([_tile_w(f("rw_w1")[0]), _tile_w(f("rw_a1")[0]), _tile_w(g1)], axis=0)
    g2 = np.zeros((512, D), np.float32)
    g2[:480] = f("rw_g2")[0]
    sw = np.concatenate([f("rw_w2")[0][None], f("rw_a2")[0][None], g2.reshape(4, 128, D)], axis=0)
    sh["w_sw"] = np.ascontiguousarray(sw.reshape(6, 128, 32, 128).transpose(2, 1, 0, 3)).reshape(32, 128, 6 * 128)
    sh["w_o1"] = _tile_w(f("rw_w_o")[0])
    sh["w_up"] = np.stack([_tile_w(f("ffn_w_up")[l]) for l in range(2)])
    wd = f("ffn_w_down")
    sh["w_dnA"] = np.stack([np.stack([_tile_w(wd[l][g * 2048:(g + 1) * 2048]) for g in range(5)]) for l in range(2)])
    sh["w_dnB"] = np.stack([_tile_w(wd[l][80 * 128:]) for l in range(2)])
    sh["w_pg"] = np.stack([_tile_w(f("ple_w_gate")[l]) for l in range(2)])
    sh["w_pp"] = np.stack([_tile_w(f("ple_w_proj")[l]) for l in range(2)])
    vs = [f("norm_mix")[0], f("norm_mix")[1], f("norm_ffn")[0], f("norm_ffn")[1], f("norm_ple")[0], f("norm_ple")[1],
          f("norm_final"), f("hg_lb_logits")[0], f("hg_lb_logits")[1]]
    vs += [f("rw_mu")[0][j] for j in range(6)]
    vs += [f("rw_w0")[0], f("rw_a0")[0], f("rw_k_k")[0], f("rw_k_a")[0], f("rw_lnx_w")[0], f("rw_lnx_b")[0],
           f("rw_r_k")[0].reshape(-1)]
    sh["vec"] = np.ascontiguousarray(np.stack([_vecfm(v) for v in vs], axis=1)).reshape(128, NV * 32)
    sh["gnorm"] = np.ascontiguousarray(f("hg_gnorm")[0].reshape(128, 1))
    cv = []
    for l in range(2):
        a = np.concatenate([f("ffn_conv_w")[l], f("ffn_conv_b")[l][None]], axis=0)
        cv.append(a.reshape(4, FCH, 128).transpose(2, 1, 0))
    sh["convp"] = np.ascontiguousarray(np.stack(cv, axis=1)).reshape(128, 2 * FCH * 4)
    sh.update(_consts())
    return sh


def prep_core(inp, c, NPT):
    f = lambda k: np.asarray(inp[k], np.float32)
    b = c % 2
    m = {}
    xp = f("x_prompt")[b]
    m["xT_p"] = np.stack([_fm(xp[i * T:(i + 1) * T]) for i in range(NPT)])
    xs = f("x_sample")[RS * c:RS * c + RS].reshape(RS * LS, D)
    m["xT_s"] = _fm(xs)
    pp = f("p_prompt")[:, b]
    m["pT_p"] = np.stack([np.stack([_fm(pp[l, i * T:(i + 1) * T]) for i in range(NPT)]) for l in range(2)])
    ps = f("p_sample")[:, RS * c:RS * c + RS].reshape(2, RS * LS, 256)
    m["pT_s"] = np.stack([_fm(ps[l]) for l in range(2)])
    hg = f("state_hgrn")[0, RS * c:RS * c + RS]
    m["st_hg_s"] = np.ascontiguousarray(hg.transpose(1, 2, 0, 3)).reshape(32, 128, RS * 128)
    rw = f("state_rwkv")[0, RS * c:RS * c + RS].reshape(RS, 32, 2, 64, 64)
    m["st_rw_s"] = np.ascontiguousarray(rw.transpose(1, 2, 4, 0, 3)).reshape(32, 128, RS * 64)
    shf = f("state_shift")[0, RS * c:RS * c + RS].reshape(RS, 32, 128)
    m["st_sh_s"] = np.ascontiguousarray(shf.transpose(2, 0, 1)).reshape(128, RS * 32)
    cvs = f("state_ffn_conv")[:, RS * c:RS * c + RS].reshape(2, RS, 2, FCH, 128)
    m["st_cv_s"] = np.ascontiguousarray(cvs.transpose(0, 4, 3, 1, 2)).reshape(2, 128, FCH * RS * 2)
    return m


_PROG_CACHE = {}


def run_device(inputs, NPT, stop_after=None, trace=False):
    key = (NPT, stop_after)
    if key not in _PROG_CACHE:
        _PROG_CACHE[key] = build_program(NPT, stop_after)
    nc = _PROG_CACHE[key]
    sh = prep_shared(inputs)
    in_maps = []
    for c in range(8):
        m = dict(sh)
        m.update(prep_core(inputs, c, NPT))
        in_maps.append(m)
    res = run_bass_kernel_spmd(nc, in_maps, core_ids=list(range(8)), **({"trace": True} if trace else {}))
    return res


def assemble(results, NPT):
    B, SEQ = 2, NPT * T
    y_p = np.zeros((B, SEQ, D), np.float32)
    y_s = np.zeros((32, LS, D), np.float32)
    hg_p = np.zeros((1, B, 32, 128, 128), np.float32)
    rw_p = np.zeros((1, B, 64, 64, 64), np.float32)
    sh_p = np.zeros((1, B, D), np.float32)
    cv_p = np.zeros((2, B, 2, DFF), np.float32)
    hg_s = np.zeros((1, 32, 32, 128, 128), np.float32)
    rw_s = np.zeros((1, 32, 64, 64, 64), np.float32)
    sh_s = np.zeros((1, 32, D), np.float32)
    cv_s = np.zeros((2, 32, 2, DFF), np.float32)
    for c in range(8):
        r = results[c]
        if c < 2:
            b = c
            y_p[b] = r["yT_p"].reshape(NPT, 128, KC, T).transpose(0, 3, 2, 1).reshape(SEQ, D)
            hg_p[0, b] = r["o_hg_p"]
            rw_p[0, b] = r["o_rw_p"].reshape(32, 2, 64, 64).transpose(0, 1, 3, 2).reshape(64, 64, 64)
            sh_p[0, b] = r["o_sh_p"].T.reshape(D)
            cv_p[:, b] = r["o_cv_p"].reshape(2, 128, FCH, 2).transpose(0, 3, 2, 1).reshape(2, 2, DFF)
        sl = slice(RS * c, RS * c + RS)
        y_s[sl] = r["yT_s"].reshape(128, KC, RS * LS).transpose(2, 1, 0).reshape(RS, LS, D)
        hg_s[0, sl] = r["o_hg_s"].reshape(32, 128, RS, 128).transpose(2, 0, 1, 3)
        rw_s[0, sl] = r["o_rw_s"].reshape(32, 2, 64, RS, 64).transpose(3, 0, 1, 4, 2).reshape(RS, 64, 64, 64)
        sh_s[0, sl] = r["o_sh_s"].reshape(128, RS, 32).transpose(1, 2, 0).reshape(RS, D)
        cv_s[:, sl] = r["o_cv_s"].reshape(2, 128, FCH, RS, 2).transpose(0, 3, 4, 2, 1).reshape(2, RS, 2, DFF)
    return (y_p, y_s, hg_p, rw_p, sh_p, cv_p, hg_s, rw_s, sh_s, cv_s)


def kernel(**inputs):
    NPT = 8
    res = run_device(inputs, NPT)
    return assemble(res.results, NPT)
```
